# Optimizing a Trainium2 kernel written in Bass

```python
import jax, jax.numpy as jnp
from jax import lax
import numpy as np

D_MODEL = 1024
BATCH = 8
SEQ = 4096
DEPTH = 2

N_EVEN = (DEPTH + 1) // 2
N_ODD = DEPTH // 2

RET_HEADS = 4
RET_DK = D_MODEL // (2 * RET_HEADS)
RET_DV = D_MODEL // (2 * RET_HEADS)
RET_CHUNK = 128
ROPE_THETA = 10000.0

LRU_WIDTH = D_MODEL // 2
LRU_BLOCKS = 8
LRU_BLOCK = LRU_WIDTH // LRU_BLOCKS
LRU_C = 8.0
CONV_K = 4

GDN_HEADS = 8
GDN_DK = D_MODEL // GDN_HEADS
GDN_DV = D_MODEL // GDN_HEADS
GDN_CHUNK = 64

D_FF = 4 * D_MODEL
EPS = 1e-6

RET_QK = RET_HEADS * RET_DK
RET_V = RET_HEADS * RET_DV
EVEN_SPLITS = (RET_QK, RET_QK, RET_V, RET_V, LRU_WIDTH, LRU_WIDTH)
EVEN_IN = sum(EVEN_SPLITS)
EVEN_MIX = RET_V + LRU_WIDTH
GDN_K = GDN_HEADS * GDN_DK
GDN_V = GDN_HEADS * GDN_DV
GDN_CONV_DIM = 2 * GDN_K + GDN_V
ODD_SPLITS = (GDN_K, GDN_K, GDN_V, GDN_V, GDN_HEADS, GDN_HEADS)
ODD_IN = sum(ODD_SPLITS)

kernel_name = "hybrid_retention_rglru_gdn_trunk"


def _split(p, sizes):
    offs = np.cumsum(sizes)[:-1].tolist()
    return jnp.split(p, offs, axis=-1)


def rms_norm(x, w):
    xf = x.astype(jnp.float32)
    return xf * lax.rsqrt(jnp.mean(xf * xf, axis=-1, keepdims=True) + EPS) * w.astype(jnp.float32)


def head_rms(x):
    return x * lax.rsqrt(jnp.mean(x * x, axis=-1, keepdims=True) + EPS)


def causal_depthwise_conv(x, w, b=None):
    K, C = w.shape
    y = lax.conv_general_dilated(x, w[:, None, :].astype(x.dtype), window_strides=(1,),
                                 padding=[(K - 1, 0)], dimension_numbers=('NWC', 'WIO', 'NWC'),
                                 feature_group_count=C)
    if b is not None:
        y = y + b.astype(x.dtype)
    return y


def rope(x, pos):
    half = x.shape[-1] // 2
    inv = ROPE_THETA ** (-jnp.arange(half, dtype=jnp.float32) / half)
    ang = pos.astype(jnp.float32)[:, None] * inv[None, :]
    cos = jnp.cos(ang)[None, :, None, :]
    sin = jnp.sin(ang)[None, :, None, :]
    x1, x2 = x[..., :half], x[..., half:]
    return jnp.concatenate([x1 * cos - x2 * sin, x1 * sin + x2 * cos], axis=-1)


def retention(q, k, v):
    B, T, H, dk = q.shape
    dv = v.shape[-1]
    C = RET_CHUNK
    N = T // C
    log_gamma = jnp.log1p(-jnp.exp2(-5.0 - jnp.arange(H, dtype=jnp.float32)))
    k = k * (dk ** -0.5)
    q = q.reshape(B, N, C, H, dk)
    k = k.reshape(B, N, C, H, dk)
    v = v.reshape(B, N, C, H, dv)
    idx = jnp.arange(C, dtype=jnp.float32)
    diff = idx[:, None] - idx[None, :]
    causal = diff >= 0
    decay = jnp.where(causal, jnp.exp(log_gamma[:, None, None] * jnp.where(causal, diff, 0.0)), 0.0)
    scores = jnp.einsum('bnihd,bnjhd->bnhij', q, k) * decay
    o_intra = jnp.einsum('bnhij,bnjhe->bnihe', scores, v)
    q_decay = jnp.exp(log_gamma[None, :] * (idx[:, None] + 1.0))
    k_decay = jnp.exp(log_gamma[None, :] * (C - 1.0 - idx[:, None]))
    chunk_gamma = jnp.exp(log_gamma * C)
    kv = jnp.einsum('bnjhd,bnjhe->bnhde', k * k_decay[:, :, None], v)

    def step(S, kv_n):
        return chunk_gamma[None, :, None, None] * S + kv_n, S

    S0 = jnp.zeros((B, H, dk, dv), jnp.float32)
    _, S_prev = lax.scan(step, S0, jnp.moveaxis(kv, 1, 0))
    S_prev = jnp.moveaxis(S_prev, 0, 1)
    o_inter = jnp.einsum('bnihd,bnhde->bnihe', q * q_decay[:, :, None], S_prev)
    return (o_intra + o_inter).reshape(B, T, H, dv)


def rg_lru(x, w_r, b_r, w_i, b_i, lam):
    B, T, W = x.shape
    xb = x.reshape(B, T, LRU_BLOCKS, LRU_BLOCK)
    r = jax.nn.sigmoid(jnp.einsum('btnd,nde->btne', xb, w_r.astype(jnp.float32)).reshape(B, T, W)
                       + b_r.astype(jnp.float32))
    i = jax.nn.sigmoid(jnp.einsum('btnd,nde->btne', xb, w_i.astype(jnp.float32)).reshape(B, T, W)
                       + b_i.astype(jnp.float32))
    log_a = -LRU_C * r * jax.nn.softplus(-lam.astype(jnp.float32))
    a = jnp.exp(log_a)
    mult = jnp.sqrt(-jnp.expm1(2.0 * log_a))
    mult = jnp.where(jnp.arange(T)[None, :, None] == 0, 1.0, mult)
    u = x * i * mult

    def combine(left, right):
        a1, b1 = left
        a2, b2 = right
        return a1 * a2, a2 * b1 + b2

    _, h = lax.associative_scan(combine, (a, u), axis=1)
    return h


def gated_delta_rule(q, k, v, g, beta):
    B, T, H, dk = q.shape
    dv = v.shape[-1]
    C = GDN_CHUNK
    N = T // C
    q = q * (dk ** -0.5)
    to_chunks = lambda t: jnp.moveaxis(t.reshape((B, N, C, H) + t.shape[3:]), 3, 1)
    q, k, v, g, beta = map(to_chunks, (q, k, v, g, beta))
    G = jnp.cumsum(g, axis=-1)
    tri_incl = jnp.tril(jnp.ones((C, C), bool))
    tri_strict = jnp.tril(jnp.ones((C, C), bool), k=-1)
    gdiff = G[..., :, None] - G[..., None, :]
    decay_incl = jnp.where(tri_incl, jnp.exp(jnp.where(tri_incl, gdiff, 0.0)), 0.0)
    decay_strict = jnp.where(tri_strict, decay_incl, 0.0)
    k_beta = k * beta[..., None]
    v_beta = v * beta[..., None]
    L = jnp.einsum('bhnid,bhnjd->bhnij', k_beta, k) * decay_strict
    A = jnp.eye(C, dtype=jnp.float32) + L
    u = lax.linalg.triangular_solve(A, v_beta, left_side=True, lower=True)
    w = lax.linalg.triangular_solve(A, k_beta * jnp.exp(G)[..., None], left_side=True, lower=True)
    qk = jnp.einsum('bhnid,bhnjd->bhnij', q, k) * decay_incl
    q_g = q * jnp.exp(G)[..., None]
    k_g = k * jnp.exp(G[..., -1:] - G)[..., None]
    chunk_decay = jnp.exp(G[..., -1])

    def step(S, xs):
        u_n, w_n, qk_n, qg_n, kg_n, cd_n = xs
        v_new = u_n - jnp.einsum('bhcd,bhde->bhce', w_n, S)
        o = jnp.einsum('bhcd,bhde->bhce', qg_n, S) + jnp.einsum('bhij,bhje->bhie', qk_n, v_new)
        S = S * cd_n[..., None, None] + jnp.einsum('bhcd,bhce->bhde', kg_n, v_new)
        return S, o

    xs = tuple(jnp.moveaxis(t, 2, 0) for t in (u, w, qk, q_g, k_g, chunk_decay))
    S0 = jnp.zeros((B, H, dk, dv), jnp.float32)
    _, o = lax.scan(step, S0, xs)
    return jnp.transpose(o, (1, 0, 3, 2, 4)).reshape(B, T, H, dv)


def retention_rglru_mixer(h, pos, w_in, lru_conv_w, lru_conv_b, lru_w_r, lru_b_r,
                          lru_w_i, lru_b_i, lru_lambda, w_out):
    B, T, _ = h.shape
    p = h @ w_in.astype(jnp.float32)
    q, k, v, g_ret, x_lru, y_lru = _split(p, EVEN_SPLITS)
    q = rope(q.reshape(B, T, RET_HEADS, RET_DK), pos)
    k = rope(k.reshape(B, T, RET_HEADS, RET_DK), pos)
    v = v.reshape(B, T, RET_HEADS, RET_DV)
    o_ret = head_rms(retention(q, k, v)).reshape(B, T, RET_V) * jax.nn.silu(g_ret)
    x_lru = causal_depthwise_conv(x_lru, lru_conv_w, lru_conv_b)
    o_lru = rg_lru(x_lru, lru_w_r, lru_b_r, lru_w_i, lru_b_i, lru_lambda) * jax.nn.gelu(y_lru)
    return jnp.concatenate([o_ret, o_lru], axis=-1) @ w_out.astype(jnp.float32)


def gated_deltanet_mixer(h, w_in, conv_w, a_log, dt_bias, norm_w, w_out):
    B, T, _ = h.shape
    p = h @ w_in.astype(jnp.float32)
    qkv, z, b, a = _split(p, (GDN_CONV_DIM, GDN_V, GDN_HEADS, GDN_HEADS))
    qkv = jax.nn.silu(causal_depthwise_conv(qkv, conv_w))
    q, k, v = _split(qkv, (GDN_K, GDN_K, GDN_V))
    q = q.reshape(B, T, GDN_HEADS, GDN_DK)
    k = k.reshape(B, T, GDN_HEADS, GDN_DK)
    v = v.reshape(B, T, GDN_HEADS, GDN_DV)
    q = q * lax.rsqrt(jnp.sum(q * q, axis=-1, keepdims=True) + EPS)
    k = k * lax.rsqrt(jnp.sum(k * k, axis=-1, keepdims=True) + EPS)
    beta = jax.nn.sigmoid(b)
    g = -jnp.exp(a_log.astype(jnp.float32)) * jax.nn.softplus(a + dt_bias.astype(jnp.float32))
    o = gated_delta_rule(q, k, v, g, beta)
    o = head_rms(o) * norm_w.astype(jnp.float32) * jax.nn.silu(z.reshape(B, T, GDN_HEADS, GDN_DV))
    return o.reshape(B, T, GDN_V) @ w_out.astype(jnp.float32)


def squared_relu_mlp(h, w_up, w_down):
    return jnp.square(jax.nn.relu(h @ w_up.astype(jnp.float32))) @ w_down.astype(jnp.float32)


def setup_inputs(seed: int = 0) -> dict:
    key = jax.random.key(seed)
    ks = jax.random.split(key, 24)
    f32 = jnp.float32
    nrm = lambda k, shape, scale: jax.random.normal(k, shape, f32) * scale
    x = nrm(ks[0], (BATCH, SEQ, D_MODEL), 1.0)
    mixer_norm_w = 1.0 + nrm(ks[1], (DEPTH, D_MODEL), 0.02)
    mlp_norm_w = 1.0 + nrm(ks[2], (DEPTH, D_MODEL), 0.02)
    final_norm_w = 1.0 + nrm(ks[3], (D_MODEL,), 0.02)
    w_in_even = nrm(ks[4], (N_EVEN, D_MODEL, EVEN_IN), D_MODEL ** -0.5)
    lru_conv_w = nrm(ks[5], (N_EVEN, CONV_K, LRU_WIDTH), CONV_K ** -0.5)
    lru_conv_b = nrm(ks[6], (N_EVEN, LRU_WIDTH), 0.01)
    lru_w_r = nrm(ks[7], (N_EVEN, LRU_BLOCKS, LRU_BLOCK, LRU_BLOCK), LRU_BLOCK ** -0.5)
    lru_b_r = nrm(ks[8], (N_EVEN, LRU_WIDTH), 0.01)
    lru_w_i = nrm(ks[9], (N_EVEN, LRU_BLOCKS, LRU_BLOCK, LRU_BLOCK), LRU_BLOCK ** -0.5)
    lru_b_i = nrm(ks[10], (N_EVEN, LRU_WIDTH), 0.01)
    a_c = jax.random.uniform(ks[11], (N_EVEN, LRU_WIDTH), f32, 0.9, 0.999)
    s = a_c ** (1.0 / LRU_C)
    lru_lambda = jnp.log(s) - jnp.log1p(-s)
    w_out_even = nrm(ks[12], (N_EVEN, EVEN_MIX, D_MODEL), EVEN_MIX ** -0.5)
    w_in_odd = nrm(ks[13], (N_ODD, D_MODEL, ODD_IN), D_MODEL ** -0.5)
    gdn_conv_w = nrm(ks[14], (N_ODD, CONV_K, GDN_CONV_DIM), CONV_K ** -0.5)
    gdn_a_log = jnp.log(jax.random.uniform(ks[15], (N_ODD, GDN_HEADS), f32, 1.0, 16.0))
    dt = jnp.exp(jax.random.uniform(ks[16], (N_ODD, GDN_HEADS), f32,
                                    float(np.log(1e-3)), float(np.log(1e-1))))
    gdn_dt_bias = dt + jnp.log(-jnp.expm1(-dt))
    gdn_norm_w = 1.0 + nrm(ks[17], (N_ODD, GDN_DV), 0.02)
    w_out_odd = nrm(ks[18], (N_ODD, GDN_V, D_MODEL), GDN_V ** -0.5)
    w_up = nrm(ks[19], (DEPTH, D_MODEL, D_FF), D_MODEL ** -0.5)
    w_down = nrm(ks[20], (DEPTH, D_FF, D_MODEL), D_FF ** -0.5)
    return {"x": x, "mixer_norm_w": mixer_norm_w, "mlp_norm_w": mlp_norm_w,
            "final_norm_w": final_norm_w, "w_in_even": w_in_even, "lru_conv_w": lru_conv_w,
            "lru_conv_b": lru_conv_b, "lru_w_r": lru_w_r, "lru_b_r": lru_b_r,
            "lru_w_i": lru_w_i, "lru_b_i": lru_b_i, "lru_lambda": lru_lambda,
            "w_out_even": w_out_even, "w_in_odd": w_in_odd, "gdn_conv_w": gdn_conv_w,
            "gdn_a_log": gdn_a_log, "gdn_dt_bias": gdn_dt_bias, "gdn_norm_w": gdn_norm_w,
            "w_out_odd": w_out_odd, "w_up": w_up, "w_down": w_down}


def reference(x, mixer_norm_w, mlp_norm_w, final_norm_w, w_in_even, lru_conv_w, lru_conv_b,
              lru_w_r, lru_b_r, lru_w_i, lru_b_i, lru_lambda, w_out_even, w_in_odd,
              gdn_conv_w, gdn_a_log, gdn_dt_bias, gdn_norm_w, w_out_odd, w_up, w_down):
    pos = jnp.arange(x.shape[1], dtype=jnp.int32)
    for layer in range(DEPTH):
        j = layer // 2
        h = rms_norm(x, mixer_norm_w[layer])
        if layer % 2 == 0:
            mix = retention_rglru_mixer(h, pos, w_in_even[j], lru_conv_w[j], lru_conv_b[j],
                                        lru_w_r[j], lru_b_r[j], lru_w_i[j], lru_b_i[j],
                                        lru_lambda[j], w_out_even[j])
        else:
            mix = gated_deltanet_mixer(h, w_in_odd[j], gdn_conv_w[j], gdn_a_log[j],
                                       gdn_dt_bias[j], gdn_norm_w[j], w_out_odd[j])
        x = x + mix.astype(x.dtype)
        x = x + squared_relu_mlp(rms_norm(x, mlp_norm_w[layer]), w_up[layer], w_down[layer]).astype(x.dtype)
    return rms_norm(x, final_norm_w).astype(x.dtype)
```

```python
import math
from contextlib import ExitStack
import numpy as np
import concourse.bass as bass
import concourse.mybir as mybir
from concourse.bass_utils import run_bass_kernel_spmd

F32 = mybir.dt.float32
BF16 = mybir.dt.bfloat16
I32 = mybir.dt.int32
ALU = mybir.AluOpType
AF = mybir.ActivationFunctionType
AX = mybir.AxisListType

T = 4096
D = 1024
TB = 512
NBLK = T // TB
EPS = 1e-6


class Buf:
    __slots__ = ("name", "w", "r", "parts", "dsem", "depoch")

    def __init__(self, name="", parts=None):
        self.name = name
        self.w = None
        self.r = {}
        self.parts = parts
        self.dsem = None
        self.depoch = -1


def _flat(bufs):
    out = []
    for b in bufs:
        if b.parts is not None:
            out.extend(b.parts)
        else:
            out.append(b)
    return out


class View:
    __slots__ = ("ap", "buf")

    def __init__(self, ap, buf):
        self.ap = ap
        self.buf = buf


class _Sub:
    __slots__ = ("t", "buf")

    def __init__(self, t, buf):
        self.t = t
        self.buf = buf

    def __getitem__(self, idx):
        return View(self.t[idx], self.buf)


class Tile:
    def __init__(self, t, name):
        self.t = t
        self.name = name
        self.buf = Buf(name)
        self.subs = {}

    def __getitem__(self, idx):
        return View(self.t[idx], self.buf)

    def at(self, key):
        b = self.subs.get(key)
        if b is None:
            b = self.subs[key] = Buf(f"{self.name}.{key}")
        return _Sub(self.t, b)

    def v(self, ap, key=None):
        return View(ap, self.buf if key is None else self.at(key).buf)


class KB:
    COMPUTE = ("pe", "dve", "act", "pool")
    DMA_STREAMS = tuple(f"d{i}" for i in range(28))

    def __init__(self, nc):
        self.nc = nc
        self.ops = {e: [] for e in ("pe", "dve", "act", "pool", "sp")}
        self.cnt = {}
        self.seen = {e: {} for e in self.ops}
        self.semkeys = list(self.COMPUTE) + list(self.DMA_STREAMS)
        for k in self.semkeys:
            self.cnt[k] = 0
        self.sems = {}
        self.nins = 0
        self.nwaits = 0
        self.epoch = 0
        self.dnext = 0

    def dma_sem_for(self, buf):
        if buf.depoch != self.epoch:
            assert self.dnext < len(self.DMA_STREAMS), "out of DMA semaphores in this phase"
            buf.dsem = self.DMA_STREAMS[self.dnext]
            buf.depoch = self.epoch
            self.dnext += 1
        return buf.dsem

    def op(self, eng, fn, reads=(), writes=(), dma=None):
        reads = _flat(reads)
        writes = _flat(writes)
        deps = {}

        def add(tok, kind):
            if tok is None:
                return
            k, v = tok
            if k == eng and (eng == "pe" or kind != "raw"):
                return
            if deps.get(k, 0) < v:
                deps[k] = v

        for b in reads:
            add(b.w, "raw")
        for b in writes:
            add(b.w, "waw")
            for k, v in b.r.items():
                add((k, v), "war")
        waits = []
        seen = self.seen[eng]
        for k, v in deps.items():
            if seen.get(k, 0) >= v:
                continue
            seen[k] = v
            waits.append((k, v))
        key, inc = (dma, 16) if dma is not None else (eng, 1)
        self.cnt[key] += inc
        val = self.cnt[key]
        self.nins += 1
        self.nwaits += len(waits)
        self.ops[eng].append((waits, fn, key, inc))
        for b in reads:
            if b.r.get(key, 0) < val:
                b.r[key] = val
        for b in writes:
            b.w = (key, val)
            b.r = {}
        return (key, val)

    def barrier(self):
        for e in self.ops:
            waits = []
            for k in self.semkeys:
                v = self.cnt[k]
                if v > 0 and self.seen[e].get(k, 0) < v and k != e:
                    self.seen[e][k] = v
                    waits.append((k, v))
            if waits:
                self.ops[e].append((waits, None, None, 0))
        self.epoch += 1
        self.dnext = 0

    def emit(self):
        nc = self.nc
        with ExitStack() as st:
            for k in self.semkeys:
                self.sems[k] = st.enter_context(nc.semaphore("s_" + k))
            block = st.enter_context(nc.Block())
            sems = self.sems

            def runner(lst):
                def f(e):
                    for waits, fn, key, inc in lst:
                        for k, v in waits:
                            e.wait_ge(sems[k], v)
                        if fn is not None:
                            fn(e).then_inc(sems[key], inc)
                return f

            block.tensor(runner(self.ops["pe"]))
            block.vector(runner(self.ops["dve"]))
            block.scalar(runner(self.ops["act"]))
            block.gpsimd(runner(self.ops["pool"]))
            block.sync(runner(self.ops["sp"]))


def _bufs(views):
    return [v.buf for v in views if isinstance(v, View)]


class CX:
    def __init__(self, nc):
        self.nc = nc
        self.kb = KB(nc)
        self.psum_all = nc.alloc_psum_tensor("psum_all", [128, 4096], F32)
        self.ps = [Tile(self.psum_all[:, i * 512:(i + 1) * 512], f"ps{i}") for i in range(8)]
        self.pp = [Tile(self.psum_all[:, i * 1024:(i + 1) * 1024], f"pp{i}") for i in range(4)]
        for i in range(4):
            self.pp[i].buf = Buf(f"pp{i}", parts=(self.ps[2 * i].buf, self.ps[2 * i + 1].buf))
        self.ps_next = 0
        self.pp_next = 0
        self.pp_reserved = set()
        self.stack = None
        self.debug = False
        self.dbg_seen = set()

    def sb(self, name, shape, dtype, st=None):
        st = st or self.stack
        self.uid = getattr(self, "uid", 0) + 1
        name = f"{name}_{self.uid}"
        t = st.enter_context(self.nc.sbuf_tensor(name, list(shape), dtype))
        return Tile(t, name)

    def bank(self):
        p = self.ps[self.ps_next]
        self.ps_next = (self.ps_next + 1) % 8
        return p

    def pair(self):
        while True:
            i = self.pp_next
            self.pp_next = (self.pp_next + 1) % 4
            if i not in self.pp_reserved:
                return self.pp[i]

    def mm(self, out, lhsT, rhs, start=True, stop=True):
        self.kb.op("pe", lambda e: e.matmul(out.ap, lhsT=lhsT.ap, rhs=rhs.ap, start=start, stop=stop),
                   reads=[lhsT.buf, rhs.buf], writes=[out.buf])

    def tr(self, out, in_, ident):
        self.kb.op("pe", lambda e: e.transpose(out.ap, in_.ap, ident.ap), reads=[in_.buf, ident.buf], writes=[out.buf])

    def act(self, out, in_, func, scale=1.0, bias=None, accum=None, eng="act"):
        kw = {}
        rd = [in_.buf]
        wr = [out.buf]
        if isinstance(scale, View):
            rd.append(scale.buf)
            kw["scale"] = scale.ap
        else:
            kw["scale"] = float(scale)
        if isinstance(bias, View):
            rd.append(bias.buf)
            kw["bias"] = bias.ap
        elif bias is not None:
            kw["bias"] = float(bias)
        if accum is not None:
            wr.append(accum.buf)
            kw["accum_out"] = accum.ap
        self.kb.op("act", lambda e: e.activation(out=out.ap, in_=in_.ap, func=func, **kw), reads=rd, writes=wr)

    def tt(self, eng, out, a, b, op):
        self.kb.op(eng, lambda e: e.tensor_tensor(out=out.ap, in0=a.ap, in1=b.ap, op=op), reads=[a.buf, b.buf], writes=[out.buf])

    def ts(self, eng, out, a, s1, op0, s2=None, op1=None, accum=None):
        rd = [a.buf] + _bufs([s1, s2])
        wr = [out.buf]
        s1v = s1.ap if isinstance(s1, View) else float(s1)
        s2v = s2.ap if isinstance(s2, View) else (None if s2 is None else float(s2))
        kw = {}
        if op1 is not None:
            kw["op1"] = op1
        if accum is not None:
            kw["accum_out"] = accum.ap
            wr.append(accum.buf)
        self.kb.op(eng, lambda e: e.tensor_scalar(out=out.ap, in0=a.ap, scalar1=s1v, scalar2=s2v, op0=op0, **kw), reads=rd, writes=wr)

    def stt(self, out, a, s, b, op0, op1):
        rd = [a.buf, b.buf] + _bufs([s])
        sv = s.ap if isinstance(s, View) else float(s)
        self.kb.op("dve", lambda e: e.scalar_tensor_tensor(out=out.ap, in0=a.ap, scalar=sv, in1=b.ap, op0=op0, op1=op1),
                   reads=rd, writes=[out.buf])

    def copy(self, eng, out, a):
        if eng == "act":
            return self.act(out, a, AF.Copy)
        self.kb.op(eng, lambda e: e.tensor_copy(out=out.ap, in_=a.ap), reads=[a.buf], writes=[out.buf])

    def recip(self, out, a):
        self.kb.op("dve", lambda e: e.reciprocal(out=out.ap, in_=a.ap), reads=[a.buf], writes=[out.buf])

    def memset(self, eng, out, val):
        self.kb.op(eng, lambda e: e.memset(out.ap, val), writes=[out.buf])

    def scan(self, out, d0, d1, init, op0=ALU.mult, op1=ALU.add):
        rd = [d0.buf, d1.buf] + _bufs([init])
        iv = init.ap if isinstance(init, View) else float(init)
        self.kb.op("dve", lambda e: e.tensor_tensor_scan(out=out.ap, data0=d0.ap, data1=d1.ap, initial=iv, op0=op0, op1=op1),
                   reads=rd, writes=[out.buf])

    def iota(self, out, pattern, base=0, cm=0):
        self.kb.op("pool", lambda e: e.iota(out.ap, pattern=pattern, base=base, channel_multiplier=cm,
                                            allow_small_or_imprecise_dtypes=True), writes=[out.buf])

    def aselect(self, out, in_, pattern, base, cm, cmp, fill=0.0):
        self.kb.op("pool", lambda e: e.affine_select(out=out.ap, in_=in_.ap, pattern=pattern, base=base, channel_multiplier=cm,
                                                     compare_op=cmp, fill=fill), reads=[in_.buf], writes=[out.buf])

    def dbg(self, name, view, shape, dtype=F32):
        if getattr(self, "debug", False) is not True or name in self.dbg_seen:
            return
        self.dbg_seen.add(name)
        t = self.nc.dram_tensor("dbg_" + name, list(shape), dtype, kind="ExternalOutput")
        db = Buf("dbg")
        self.kb.op("sp", lambda e: e.dma_start(out=t.ap(), in_=view.ap), reads=[view.buf], writes=[db], dma=self.kb.dma_sem_for(db))

    def dma(self, out, in_, stream=None, eng="sp", **kw):
        assert out.buf.parts is None
        stream = self.kb.dma_sem_for(out.buf)
        return self.kb.op(eng, lambda e: e.dma_start(out=out.ap, in_=in_.ap, **kw), reads=[in_.buf], writes=[out.buf], dma=stream)


def build_consts(cx):
    st = cx.gstack
    C = {}
    C["ident_f"] = cx.sb("ident_f", [128, 128], F32, st)
    C["ident_b"] = cx.sb("ident_b", [128, 128], BF16, st)
    C["ones_f"] = cx.sb("ones_f", [128, 128], F32, st)
    C["ones_b"] = cx.sb("ones_b", [128, 128], BF16, st)
    cx.memset("pool", C["ones_f"][:], 1.0)
    cx.memset("pool", C["ones_b"][:], 1.0)
    cx.aselect(C["ident_f"][:], C["ones_f"][:], [[-1, 128]], 0, 1, ALU.is_equal)
    cx.copy("dve", C["ident_b"][:], C["ident_f"][:])
    cx.C = C
    return C


def load_cols(cx, dst, col0, rows_ap_view, nrows, stage):
    C = cx.C
    cx.dma(stage[0:nrows, :], rows_ap_view, "cst")
    pb = cx.bank()
    cx.tr(pb[:, 0:nrows], stage[0:nrows, :], C["ident_f"][0:nrows, 0:nrows])
    cx.copy("dve", dst[:, col0:col0 + nrows], pb[:, 0:nrows])


def load_w_bf16(cx, dst_view, src_view, **kw):
    cx.dma(dst_view, src_view, "wt", eng="pool", **kw)


def rms_stats(cx, xT, sqbuf, rstd, tmp):
    C = cx.C
    for c in range(8):
        cx.act(sqbuf[:, c, :], xT[:, c, :], AF.Square)
    pb = cx.bank()
    for c in range(8):
        cx.mm(pb[:], C["ones_b"][:], sqbuf[:, c, :], start=(c == 0), stop=(c == 7))
    cx.act(tmp[:], pb[:], AF.Sqrt, scale=1.0 / D, bias=cx.eps_col[:, 0:1])
    cx.recip(rstd[:], tmp[:])


def phase_mlp(cx, layer, src, dst, out_tok=None):
    kb = cx.kb
    C = cx.C
    with ExitStack() as st:
        cx.stack = st
        Wup = cx.sb("wup", [128, 8, 4096], BF16)
        Wdn = cx.sb("wdn", [128, 32, 1024], BF16)
        xT = cx.sb("xT", [128, 8, TB], F32)
        hT = cx.sb("hT", [128, 8, TB], BF16)
        hid = cx.sb("hid", [128, 32, TB], BF16)
        rl = [cx.sb(f"rl{i}", [128, TB], F32) for i in range(2)]
        rstd = cx.sb("rstd", [128, TB], F32)
        tmp = cx.sb("tmp", [128, TB], F32)
        if out_tok is not None:
            otok = [cx.sb(f"otok{i}", [128, D], F32) for i in range(2)]
        wu = cx.d["w_up"]
        wd = cx.d["w_down"]
        for c in range(8):
            for hf in range(2):
                load_w_bf16(cx, Wup[:, c, hf * 2048:(hf + 1) * 2048],
                            View(wu.ap[layer, c * 128:(c + 1) * 128, hf * 2048:(hf + 1) * 2048], wu.buf))
        wdv = wd.ap[layer].rearrange("(f p) m -> p f m", p=128)
        for f2 in range(16):
            load_w_bf16(cx, Wdn[:, 2 * f2:2 * f2 + 2, :], View(wdv[:, 2 * f2:2 * f2 + 2, :], wd.buf))
        nwc = cx.vcol["mlp_norm_w"] + 8 * layer
        srcv = src.ap.rearrange("(c p) t -> p c t", p=128)
        dstv = dst.ap.rearrange("(c p) t -> p c t", p=128)
        for b in range(NBLK):
            tsl = slice(b * TB, (b + 1) * TB)
            cx.dma(xT[:], View(srcv[:, :, tsl], src.buf), "ld")
            rms_stats(cx, xT, hT, rstd, tmp)
            for c in range(8):
                cx.stt(hT[:, c, :], xT[:, c, :], cx.vec[:, nwc + c:nwc + c + 1], rstd[:], ALU.mult, ALU.mult)
            for f in range(32):
                pb = cx.bank()
                for c in range(8):
                    cx.mm(pb[:], Wup[:, c, f * 128:(f + 1) * 128], hT[:, c, :], start=(c == 0), stop=(c == 7))
                r = rl[f % 2]
                cx.act(r[:], pb[:], AF.Relu)
                cx.tt("pool", hid.at(f)[:, f, :], r[:], r[:], ALU.mult)
            for m in range(8):
                pb = cx.bank()
                for f in range(32):
                    cx.mm(pb[:], Wdn[:, f, m * 128:(m + 1) * 128], hid.at(f)[:, f, :], start=(f == 0), stop=(f == 31))
                cx.tt("dve", xT[:, m, :], xT[:, m, :], pb[:], ALU.add)
            if out_tok is None:
                cx.dma(View(dstv[:, :, tsl], dst.buf), xT[:], "st")
            else:
                fwc = cx.vcol["final_norm_w"]
                rms_stats(cx, xT, hT, rstd, tmp)
                for c in range(8):
                    cx.stt(xT[:, c, :], xT[:, c, :], cx.vec[:, fwc + c:fwc + c + 1], rstd[:], ALU.mult, ALU.mult)
                for s in range(4):
                    ot = otok[s % 2]
                    for half in range(2):
                        pb = cx.bank()
                        for cc in range(4):
                            c = half * 4 + cc
                            cx.tr(pb[:, cc * 128:(cc + 1) * 128], xT[:, c, s * 128:(s + 1) * 128], C["ident_f"][:])
                        cx.copy("act" if half == 0 else "dve", ot[:, half * 512:(half + 1) * 512], pb[:])
                    r0 = b * TB + s * 128
                    cx.dma(View(out_tok.ap[r0:r0 + 128, :], out_tok.buf), ot[:], "st")
    kb.barrier()
    cx.stack = None


def phase_xpose_in(cx, x_tok, dst):
    C = cx.C
    with ExitStack() as st:
        cx.stack = st
        xin = [cx.sb(f"xin{i}", [128, D], F32) for i in range(2)]
        xT = [cx.sb(f"xTi{i}", [128, 8, TB], F32) for i in range(2)]
        dstv = dst.ap.rearrange("(c p) t -> p c t", p=128)
        for b in range(NBLK):
            xt = xT[b % 2]
            for s in range(4):
                xi = xin[s % 2]
                r0 = b * TB + s * 128
                cx.dma(xi[:], View(x_tok.ap[r0:r0 + 128, :], x_tok.buf), "ld")
                for half in range(2):
                    pb = cx.bank()
                    for cc in range(4):
                        c = half * 4 + cc
                        cx.tr(pb[:, cc * 128:(cc + 1) * 128], xi[:, c * 128:(c + 1) * 128], C["ident_f"][:])
                    cx.copy("act" if half == 0 else "dve", xt[:, half * 4:half * 4 + 4, s * 128:(s + 1) * 128],
                            View(pb.t[:].rearrange("p (c t) -> p c t", c=4), pb.buf))
            cx.dma(View(dstv[:, :, b * TB:(b + 1) * TB], dst.buf), xt[:], "st")
    cx.kb.barrier()
    cx.stack = None


INPUT_SHAPES = {
    "mixer_norm_w": (2, 1024), "mlp_norm_w": (2, 1024), "final_norm_w": (1024,),
    "w_in_even": (1024, 4096), "lru_conv_w": (4, 512), "lru_conv_b": (512,),
    "lru_w_r": (8, 64, 64), "lru_b_r": (512,), "lru_w_i": (8, 64, 64), "lru_b_i": (512,), "lru_lambda": (512,),
    "w_out_even": (1024, 1024), "w_in_odd": (1024, 4112), "gdn_conv_w": (4, 3072),
    "gdn_a_log": (8,), "gdn_dt_bias": (8,), "gdn_norm_w": (128,), "w_out_odd": (1024, 1024),
    "w_up": (2, 1024, 4096), "w_down": (2, 4096, 1024),
}


def build(phases=("xin", "mix0", "mlp0", "mix1", "mlp1"), debug_io=False, debug=False):
    nc = bass.Bass("TRN2", target_bir_lowering=False)
    global LAST_CX
    cx = CX(nc)
    LAST_CX = cx
    cx.debug = debug
    d = {}

    def dram(name, shape, kind):
        t = nc.dram_tensor(name, list(shape), F32, kind=kind)
        return Tile(t, name) if False else _DT(t.ap(), name)

    class _DT:
        def __init__(self, ap, name):
            self.ap = ap
            self.buf = Buf(name)

    for k, shp in INPUT_SHAPES.items():
        d[k] = dram(k, shp, "ExternalInput")
    cx.d = d
    if debug_io:
        xs_in = dram("xs_in", (D, T), "ExternalInput")
        xs_out = dram("xs_out", (D, T), "ExternalOutput")
    x_tok = dram("x", (T, D), "ExternalInput")
    out_tok = dram("out", (T, D), "ExternalOutput")
    xs = dram("xs", (D, T), "Internal")

    with ExitStack() as gst:
        cx.gstack = gst
        build_consts(cx)
        cx.vec = cx.sb("vec", [128, 176], F32, gst)
        cx.eps_col = cx.sb("eps_col", [128, 1], F32, gst)
        cx.memset("pool", cx.eps_col[:], EPS)
        stage = cx.sb("vstage", [128, 128], F32, gst)
        vcol = {}
        col = 0
        for name in ("mixer_norm_w", "mlp_norm_w", "final_norm_w", "lru_conv_w", "lru_conv_b", "lru_b_r", "lru_b_i",
                     "lru_lambda", "gdn_norm_w", "gdn_conv_w"):
            n = int(np.prod(INPUT_SHAPES[name])) // 128
            flat = d[name].ap
            if len(INPUT_SHAPES[name]) == 2:
                flat = flat.rearrange("a (r p) -> (a r) p", p=128)
            else:
                flat = flat.rearrange("(r p) -> r p", p=128)
            load_cols(cx, cx.vec, col, View(flat, d[name].buf), n, stage)
            vcol[name] = col
            col += n
        cx.vcol = vcol
        assert col <= 176, col

        if debug_io:
            for ph in phases:
                if ph == "mlp0":
                    phase_mlp(cx, 0, xs_in, xs_out)
                elif ph == "mlp1":
                    phase_mlp(cx, 1, xs_in, xs_out)
                elif ph == "mix0":
                    phase_mix0(cx, xs_in, xs_out)
                elif ph == "mix1":
                    phase_mix1(cx, xs_in, xs_out)
        else:
            if debug:
                st_ = [dram(f"stage{i}", (D, T), "ExternalOutput") for i in range(4)]
                phase_xpose_in(cx, x_tok, st_[0])
                phase_mix0(cx, st_[0], st_[1])
                phase_mlp(cx, 0, st_[1], st_[2])
                phase_mix1(cx, st_[2], st_[3])
                phase_mlp(cx, 1, st_[3], xs, out_tok=out_tok)
            else:
                xs2 = dram("xs2", (D, T), "Internal")
                phase_xpose_in(cx, x_tok, xs)
                phase_mix0(cx, xs, xs2)
                phase_mlp(cx, 0, xs2, xs)
                phase_mix1(cx, xs, xs2)
                phase_mlp(cx, 1, xs2, xs, out_tok=out_tok)
        cx.kb.barrier()
        cx.kb.emit()
    return nc


def prep_weights(inp):
    m = {}
    for k, shp in INPUT_SHAPES.items():
        a = np.asarray(inp[k], np.float32)
        if k == "w_in_even":
            w = a[0]
            q = w[:, 0:512]
            kk = w[:, 512:1024]

            def swap(m_):
                return np.ascontiguousarray(m_.reshape(1024, 4, 2, 64)[:, :, ::-1, :]).reshape(1024, 512)

            a = np.concatenate([q, kk, swap(q), swap(kk), w[:, 1024:]], axis=1)
        elif a.shape != tuple(shp):
            a = a.reshape(shp)
        m[k] = np.ascontiguousarray(a)
    return m


_NC_CACHE = {}


def kernel(**inputs):
    if "nc" not in _NC_CACHE:
        _NC_CACHE["nc"] = build()
    nc = _NC_CACHE["nc"]
    w = prep_weights(inputs)
    x = np.asarray(inputs["x"], np.float32)
    n = x.shape[0]
    in_maps = []
    for b in range(n):
        m = dict(w)
        m["x"] = np.ascontiguousarray(x[b])
        in_maps.append(m)
    res = run_bass_kernel_spmd(nc, in_maps, core_ids=list(range(n)))
    return np.stack([np.asarray(r["out"], np.float32) for r in res.results], axis=0)


TWO_PI = 2.0 * math.pi
CW1 = float(np.float32(TWO_PI))
CW2 = float(TWO_PI - float(np.float32(TWO_PI)))
PI_SAFE = 3.1415925


def sin_table(cx, out, ang, scale, tmps, shift):
    u, ki, kf, r, m = tmps
    cx.ts("dve", u[:], ang[:], shift, ALU.add, 1.0 / TWO_PI, ALU.mult)
    cx.copy("dve", ki[:], u[:])
    cx.copy("dve", kf[:], ki[:])
    cx.ts("dve", r[:], ang[:], shift, ALU.add)
    cx.stt(r[:], kf[:], -CW1, r[:], ALU.mult, ALU.add)
    cx.stt(r[:], kf[:], -CW2, r[:], ALU.mult, ALU.add)
    cx.ts("dve", m[:], r[:], math.pi, ALU.is_gt, -TWO_PI, ALU.mult)
    cx.tt("dve", r[:], r[:], m[:], ALU.add)
    cx.ts("dve", m[:], r[:], -math.pi, ALU.is_lt, TWO_PI, ALU.mult)
    cx.tt("dve", r[:], r[:], m[:], ALU.add)
    cx.ts("dve", r[:], r[:], PI_SAFE, ALU.min, -PI_SAFE, ALU.max)
    cx.act(out[:], r[:], AF.Sin, scale=scale)


def phase_mix0(cx, src, dst):
    kb = cx.kb
    C = cx.C
    d = cx.d
    LG = [math.log1p(-2.0 ** (-5 - h)) for h in range(4)]
    SC = 128 ** -0.5
    with ExitStack() as st:
        cx.stack = st
        Win = cx.sb("win0", [128, 8, 4096], BF16)
        Wout = cx.sb("wout0", [128, 8, 1024], BF16)
        wi = d["w_in_even"]
        for c in range(8):
            for hf in range(2):
                load_w_bf16(cx, Win[:, c, hf * 2048:(hf + 1) * 2048], View(wi.ap[c * 128:(c + 1) * 128, hf * 2048:(hf + 1) * 2048], wi.buf))
        wo = d["w_out_even"]
        wov = wo.ap.rearrange("(c p) m -> p c m", p=128)
        for c2 in range(4):
            load_w_bf16(cx, Wout[:, 2 * c2:2 * c2 + 2, :], View(wov[:, 2 * c2:2 * c2 + 2, :], wo.buf))
        Wr = cx.sb("wr_bd", [128, 4, 128], BF16)
        Wi_ = cx.sb("wi_bd", [128, 4, 128], BF16)
        cx.memset("pool", Wr[:], 0.0)
        cx.memset("pool", Wi_[:], 0.0)
        for nm, Wt in (("lru_w_r", Wr), ("lru_w_i", Wi_)):
            for n in range(8):
                c, hh = n // 2, n % 2
                load_w_bf16(cx, Wt[hh * 64:(hh + 1) * 64, c, hh * 64:(hh + 1) * 64], View(d[nm].ap[n], d[nm].buf))
        cc = cx.sb("cc0", [128, 8], F32)
        cx.memset("pool", cc[:, 0:1], math.log(SC))
        cx.memset("pool", cc[:, 1:2], 1.0)
        cx.memset("pool", cc[0:64, 2:3], -1.0)
        cx.memset("pool", cc[64:128, 2:3], 1.0)
        pj = cx.sb("pj", [128, 1], F32)
        cx.iota(pj[0:64, :], [[0, 1]], 0, 1)
        cx.iota(pj[64:128, :], [[0, 1]], 0, 1)
        cx.act(cc[:, 3:4], pj[:], AF.Exp, scale=-math.log(10000.0) / 64.0)
        idiff = cx.sb("idiff", [128, 128], F32)
        cx.iota(idiff[:], [[1, 128]], 0, -1)
        DT = cx.sb("DT", [128, 4, 128], F32)
        QDEC = cx.sb("QDEC", [128, 4, TB], F32)
        KDEC = cx.sb("KDEC", [128, 4, 128], F32)
        ip1 = cx.sb("ip1", [128, 128], F32)
        cx.iota(ip1[:], [[1, 128]], 1, 0)
        jr = cx.sb("jr", [128, 128], F32)
        cx.iota(jr[:], [[0, 128]], 127, -1)
        for h in range(4):
            cx.act(DT[:, h, :], idiff[:], AF.Exp, scale=LG[h], bias=cc[:, 0:1])
            for s in range(4):
                cx.act(QDEC[:, h, s * 128:(s + 1) * 128], ip1[:], AF.Exp, scale=LG[h])
            cx.act(KDEC[:, h, :], jr[:], AF.Exp, scale=LG[h], bias=cc[:, 0:1])
        cx.aselect(DT[:], DT[:], [[0, 4], [1, 128]], 0, -1, ALU.is_ge)
        CG = [math.exp(LG[h] * 128.0) for h in range(4)]
        nsp8 = cx.sb("nsp8", [128, 4], F32)
        lc = cx.vcol["lru_lambda"]
        cx.act(nsp8[:], cx.vec[:, lc:lc + 4], AF.Exp, scale=-1.0)
        cx.act(nsp8[:], nsp8[:], AF.Ln, bias=cc[:, 1:2])
        cx.ts("dve", nsp8[:], nsp8[:], -8.0, ALU.mult)
        xT = cx.sb("xT", [128, 8, TB], F32)
        hT = cx.sb("hT", [128, 8, TB], BF16)
        rstd = cx.sb("rstd", [128, TB], F32)
        tmp = cx.sb("tmp", [128, TB], F32)
        tpos = cx.sb("tpos", [128, TB], F32)
        ang = cx.sb("ang", [128, TB], F32)
        cosT = cx.sb("cosT", [128, TB], F32)
        sinS = cx.sb("sinS", [128, TB], F32)
        tu = cx.sb("tu", [128, TB], F32)
        tki = cx.sb("tki", [128, TB], I32)
        tkf = cx.sb("tkf", [128, TB], F32)
        tr_ = cx.sb("tr", [128, TB], F32)
        tm = cx.sb("tm", [128, TB], F32)
        t1 = [cx.sb(f"t1_{i}", [128, TB], F32) for i in range(2)]
        t2 = [cx.sb(f"t2_{i}", [128, TB], F32) for i in range(2)]
        t3 = cx.sb("t3", [128, TB], F32)
        QT = cx.sb("QT", [128, 4, TB], BF16)
        QD = cx.sb("QD", [128, 4, TB], BF16)
        KT = cx.sb("KT", [128, 4, TB], BF16)
        Vtok = cx.sb("Vtok", [128, 4, 512], BF16)
        GS = cx.sb("GS", [128, 4, TB], F32)
        GY = cx.sb("GY", [128, 4, TB], F32)
        MIX = cx.sb("MIX", [128, 8, TB], BF16)
        Ktok = cx.sb("Ktok", [128, 4, 128], BF16)
        Pm = cx.sb("Pm", [128, 4, 128], BF16)
        Sf = cx.sb("Sf", [128, 4, 128], F32)
        Sb = cx.sb("Sb", [128, 4, 128], BF16)
        cx.memset("pool", Sf[:], 0.0)
        cx.memset("pool", Sb[:], 0.0)
        sqo = cx.sb("sqo", [128, 512], BF16)
        on = tmp
        rso = rstd
        raw = cx.sb("raw", [128, 3 + TB], F32)
        halo = cx.sb("halo", [128, 4, 3], F32)
        cx.memset("pool", halo[:], 0.0)
        xc = t3
        xb = cx.sb("xb", [128, TB], BF16)
        gr, gi, ga, gm, gu, Hs = tu, tkf, tr_, tm, ang, tpos
        hlast = cx.sb("hlast", [128, 4], F32)
        cx.memset("pool", hlast[:], 0.0)

        nwc = cx.vcol["mixer_norm_w"]
        cwc = cx.vcol["lru_conv_w"]
        cbc = cx.vcol["lru_conv_b"]
        brc = cx.vcol["lru_b_r"]
        bic = cx.vcol["lru_b_i"]
        srcv = src.ap.rearrange("(c p) t -> p c t", p=128)
        dstv = dst.ap.rearrange("(c p) t -> p c t", p=128)

        def proj(col0):
            pb = cx.bank()
            for c in range(8):
                cx.mm(pb[:], Win[:, c, col0:col0 + 128], hT[:, c, :], start=(c == 0), stop=(c == 7))
            return pb

        for b in range(NBLK):
            tsl = slice(b * TB, (b + 1) * TB)
            cx.dma(xT[:], View(srcv[:, :, tsl], src.buf), "ld")
            cx.iota(tpos[:], [[1, TB]], b * TB, 0)
            cx.ts("dve", ang[:], tpos[:], cc[:, 3:4], ALU.mult)
            sin_table(cx, sinS, ang, cc[:, 2:3], (tu, tki, tkf, tr_, tm), 0.0)
            sin_table(cx, cosT, ang, 1.0, (tu, tki, tkf, tr_, tm), math.pi / 2)
            rms_stats(cx, xT, hT, rstd, tmp)
            for c in range(8):
                cx.stt(hT[:, c, :], xT[:, c, :], cx.vec[:, nwc + c:nwc + c + 1], rstd[:], ALU.mult, ALU.mult)
            for h in range(4):
                for which, dstT in ((0, QT), (1, KT)):
                    pa = proj(which * 512 + h * 128)
                    pswp = proj(1024 + which * 512 + h * 128)
                    a1, a2 = t1[which], t2[which]
                    cx.tt("dve", a1[:], pa[:], cosT[:], ALU.mult)
                    cx.tt("dve", a2[:], pswp[:], sinS[:], ALU.mult)
                    if which == 0:
                        cx.tt("pool", t3[:], a1[:], a2[:], ALU.add)
                        cx.copy("act", QT[:, h, :], t3[:])
                        cx.tt("pool", QD[:, h, :], t3[:], QDEC[:, h, :], ALU.mult)
                    else:
                        cx.tt("pool", KT[:, h, :], a1[:], a2[:], ALU.add)
            for s in range(4):
                pb = cx.bank()
                for c in range(8):
                    cx.mm(pb[:], hT[:, c, s * 128:(s + 1) * 128], Win[:, c, 2048:2560], start=(c == 0), stop=(c == 7))
                cx.copy("act", Vtok[:, s, :], pb[:])
            for h in range(4):
                pb = proj(2560 + h * 128)
                cx.act(GS[:, h, :], pb[:], AF.Silu)
            for c in range(4):
                pb = proj(3584 + c * 128)
                cx.act(GY[:, c, :], pb[:], AF.Gelu_apprx_tanh)
            for s in range(4):
                csl = slice(s * 128, (s + 1) * 128)
                pk = cx.bank()
                pkb = pk.t[:].bitcast(BF16)
                for h in range(4):
                    cx.tr(View(pkb[:, h * 128:(h + 1) * 128], pk.buf), KT[:, h, csl], C["ident_b"][:])
                cx.tt("dve", Ktok[:], View(pkb[:, 0:512].rearrange("p (h d) -> p h d", h=4), pk.buf), KDEC[:], ALU.mult)
                psc = cx.bank()
                for h in range(4):
                    cx.mm(psc[:, h * 128:(h + 1) * 128], KT[:, h, csl], QT[:, h, csl])
                cx.tt("dve", Pm[:], View(psc.t[:].rearrange("p (h i) -> p h i", h=4), psc.buf), DT[:], ALU.mult)
                po = cx.bank()
                for h in range(4):
                    cx.mm(po[:, h * 128:(h + 1) * 128], Vtok[:, s, h * 128:(h + 1) * 128], Pm[:, h, :], start=True, stop=False)
                    cx.mm(po[:, h * 128:(h + 1) * 128], Sb[:, h, :], QD[:, h, csl], start=False, stop=True)
                pkv = cx.bank()
                for h in range(4):
                    cx.mm(pkv[:, h * 128:(h + 1) * 128], Ktok[:, h, :], Vtok[:, s, h * 128:(h + 1) * 128])
                for h in range(4):
                    cx.stt(Sf[:, h, :], Sf[:, h, :], CG[h], pkv[:, h * 128:(h + 1) * 128], ALU.mult, ALU.add)
                cx.copy("act", Sb[:], Sf[:])
                cx.act(sqo[:], po[:], AF.Square)
                pss = cx.bank()
                cx.mm(pss[:], C["ones_b"][:], sqo[:])
                cx.act(rso[:], pss[:], AF.Sqrt, scale=1.0 / 128, bias=cx.eps_col[:, 0:1])
                cx.recip(rso[:], rso[:])
                cx.tt("dve", on[:], po[:], rso[:], ALU.mult)
                cx.tt("pool", MIX[:, 0:4, csl], View(on.t[:].rearrange("p (h i) -> p h i", h=4), on.buf), GS[:, :, csl], ALU.mult)
            for c in range(4):
                pb = proj(3072 + c * 128)
                cx.copy("act", raw[:, 0:3], halo[:, c, :])
                cx.copy("act", raw[:, 3:3 + TB], pb[:])
                cx.copy("act", halo[:, c, :], raw[:, TB:TB + 3])
                cx.ts("pool", xc[:], raw[:, 0:TB], cx.vec[:, cwc + c:cwc + c + 1], ALU.mult,
                      cx.vec[:, cbc + c:cbc + c + 1], ALU.add)
                for k in range(1, 4):
                    cx.stt(xc[:], raw[:, k:k + TB], cx.vec[:, cwc + 4 * k + c:cwc + 4 * k + c + 1], xc[:], ALU.mult, ALU.add)
                cx.copy("pool", xb[:], xc[:])
                pr = cx.bank()
                cx.mm(pr[:], Wr[:, c, :], xb[:])
                pi_ = cx.bank()
                cx.mm(pi_[:], Wi_[:, c, :], xb[:])
                cx.act(gr[:], pr[:], AF.Sigmoid, bias=cx.vec[:, brc + c:brc + c + 1])
                cx.act(gi[:], pi_[:], AF.Sigmoid, bias=cx.vec[:, bic + c:bic + c + 1])
                cx.act(ga[:], gr[:], AF.Exp, scale=nsp8[:, c:c + 1])
                cx.tt("pool", gm[:], ga[:], ga[:], ALU.mult)
                cx.act(gm[:], gm[:], AF.Sqrt, scale=-1.0, bias=cc[:, 1:2])
                if b == 0:
                    cx.memset("pool", gm[:, 0:1], 1.0)
                cx.tt("pool", gu[:], xc[:], gi[:], ALU.mult)
                cx.tt("pool", gu[:], gu[:], gm[:], ALU.mult)
                cx.scan(Hs[:], ga[:], gu[:], hlast[:, c:c + 1])
                cx.copy("act", hlast[:, c:c + 1], Hs[:, TB - 1:TB])
                cx.tt("pool", MIX[:, 4 + c, :], Hs[:], GY[:, c, :], ALU.mult)
            for m in range(8):
                pb = cx.bank()
                for c in range(8):
                    cx.mm(pb[:], Wout[:, c, m * 128:(m + 1) * 128], MIX[:, c, :], start=(c == 0), stop=(c == 7))
                cx.tt("dve", xT[:, m, :], xT[:, m, :], pb[:], ALU.add)
            cx.dma(View(dstv[:, :, tsl], dst.buf), xT[:], "st")
    kb.barrier()
    cx.stack = None


def phase_mix1(cx, src, dst):
    kb = cx.kb
    C = cx.C
    d = cx.d
    SC = 128 ** -0.5
    H = 8
    with ExitStack() as st:
        cx.stack = st
        Win = cx.sb("win1", [128, 8, 4112], BF16)
        Wout = cx.sb("wout1", [128, 8, 1024], BF16)
        wi = d["w_in_odd"]
        for c in range(8):
            load_w_bf16(cx, Win[:, c, 0:2048], View(wi.ap[c * 128:(c + 1) * 128, 0:2048], wi.buf))
            load_w_bf16(cx, Win[:, c, 2048:4096], View(wi.ap[c * 128:(c + 1) * 128, 2048:4096], wi.buf))
            load_w_bf16(cx, Win[:, c, 4096:4112], View(wi.ap[c * 128:(c + 1) * 128, 4096:4112], wi.buf))
        wo = d["w_out_odd"]
        wov = wo.ap.rearrange("(c p) m -> p c m", p=128)
        for c2 in range(4):
            load_w_bf16(cx, Wout[:, 2 * c2:2 * c2 + 2, :], View(wov[:, 2 * c2:2 * c2 + 2, :], wo.buf))
        gnw = cx.vcol["gdn_norm_w"]
        for c in range(8):
            cx.ts("pool", Wout[:, c, :], Wout[:, c, :], cx.vec[:, gnw:gnw + 1], ALU.mult, 0.0, ALU.add)
        cc = cx.sb("cc1", [128, 4], F32)
        cx.memset("pool", cc[:, 0:1], 1.0)
        cx.memset("pool", cc[:, 1:2], 128.0 * EPS)
        Tri = cx.sb("Tri", [128, 128], F32)
        Ust = cx.sb("Ust", [128, 128], F32)
        cx.aselect(Tri[:], C["ones_f"][:], [[1, 128]], 0, -1, ALU.is_ge)
        cx.aselect(Ust[:], C["ones_f"][:], [[-1, 128]], -1, 1, ALU.is_ge)
        I4 = cx.sb("I4", [128, 4, 128], BF16)
        for h in range(4):
            cx.copy("dve", I4[:, h, :], C["ident_f"][:])
        I8 = cx.sb("I8", [128, H, 128], BF16)
        for h in range(H):
            cx.copy("dve", I8[:, h, :], C["ident_f"][:])
        MK = cx.sb("MK", [128, 7, 128], BF16)
        MKT1 = cx.sb("MKT1", [128, 128], BF16)
        ub = cx.sb("ub", [64, 128], F32)
        lb = cx.sb("lb", [64, 128], F32)
        for lv in range(7):
            n = 1 << lv
            nb = 128 // (2 * n)
            cx.aselect(ub[0:nb, :], C["ones_f"][0:nb, :], [[1, 128]], -n, -2 * n, ALU.is_ge)
            cx.aselect(ub[0:nb, :], ub[0:nb, :], [[-1, 128]], 2 * n - 1, 2 * n, ALU.is_ge)
            cx.aselect(lb[0:nb, :], C["ones_f"][0:nb, :], [[1, 128]], 0, -2 * n, ALU.is_ge)
            cx.aselect(lb[0:nb, :], lb[0:nb, :], [[-1, 128]], n - 1, 2 * n, ALU.is_ge)
            pm_ = cx.bank()
            cx.mm(pm_[:, 0:128], ub[0:nb, :], lb[0:nb, :])
            cx.copy("dve", MK[:, lv, :], pm_[:, 0:128])
            if lv == 0:
                pm2 = cx.bank()
                cx.mm(pm2[:, 0:128], lb[0:nb, :], ub[0:nb, :])
                cx.copy("dve", MKT1[:], pm2[:, 0:128])
        ALB = cx.sb("ALB", [128, 8], F32)
        DTB = cx.sb("DTB", [128, 8], F32)
        cx.dma(ALB[:], View(d["gdn_a_log"].ap.partition_broadcast(128), d["gdn_a_log"].buf), "cst")
        cx.dma(DTB[:], View(d["gdn_dt_bias"].ap.partition_broadcast(128), d["gdn_dt_bias"].buf), "cst")
        negA = cx.sb("negA", [128, 8], F32)
        cx.act(negA[:], ALB[:], AF.Exp)
        cx.ts("dve", negA[:], negA[:], -1.0, ALU.mult)
        xT = cx.sb("xT", [128, 8, TB], F32)
        hT = cx.sb("hT", [128, 8, TB], BF16)
        rstd = cx.sb("rstd", [128, TB], F32)
        tmp = cx.sb("tmp", [128, TB], F32)
        QT = cx.sb("QTg", [128, H, TB], BF16)
        KT = cx.sb("KTg", [128, H, TB], BF16)
        VT = cx.sb("VTg", [128, H, TB], BF16)
        OT = VT
        raw = cx.sb("raw1", [128, 3 + TB], F32)
        halo = cx.sb("halo1", [128, 24, 3], F32)
        cx.memset("pool", halo[:], 0.0)
        acc = cx.sb("acc1", [128, TB], F32)
        sil = cx.sb("sil1", [128, TB], F32)
        sqb = cx.sb("sqb1", [128, TB], BF16)
        rn = tmp
        sm = {n: cx.sb("sm_" + n, [128, 8], F32) for n in ("beta", "nbeta", "x", "ax", "e1", "g", "eG", "bG", "k2", "cd", "ss", "rs")}
        gUT = cx.sb("gUT", [128, H, 128], F32)
        DS = cx.sb("DS", [128, H, 128], BF16)
        DtI = cx.sb("DtI", [128, H, 128], BF16)
        KG = cx.sb("KG", [128, H, 128], BF16)
        KG2 = cx.sb("KG2", [128, H, 128], BF16)
        Vb = cx.sb("Vb", [128, H, 128], BF16)
        tX = gUT
        Xb = cx.sb("Xb", [128, H, 128], BF16)
        Zb = cx.sb("Zb", [128, H, 128], BF16)
        Xo = cx.sb("Xo", [128, H, 128], BF16)
        Wl = cx.sb("Wl", [128, H, 128], BF16)
        P1b = cx.sb("P1b", [128, H, 128], BF16)
        Yb = cx.sb("Yb", [128, H, 128], BF16)
        wTn = cx.sb("wTn", [128, H, 128], BF16)
        vnew = cx.sb("vnew", [128, H, 128], BF16)
        Pm = cx.sb("Pm1", [128, H, 128], BF16)
        o2s = cx.sb("o2s", [128, H, 128], F32)
        ot = cx.sb("ot", [128, H, 128], F32)
        zs = cx.sb("zs", [128, H, 128], F32)
        of = cx.sb("of", [128, H, 128], BF16)
        Sf = cx.sb("Sf1", [128, H, 128], F32)
        Sb = cx.sb("Sb1", [128, H, 128], BF16)
        cx.memset("pool", Sf[:], 0.0)
        cx.memset("pool", Sb[:], 0.0)

        nwc = cx.vcol["mixer_norm_w"] + 8
        cwc = cx.vcol["gdn_conv_w"]
        srcv = src.ap.rearrange("(c p) t -> p c t", p=128)
        dstv = dst.ap.rearrange("(c p) t -> p c t", p=128)

        def bc(view, n):
            return View(view.ap.unsqueeze(2).broadcast_to([128, view.ap.shape[1], n]), view.buf)

        def bch(view, n):
            return View(view.ap.unsqueeze(1).broadcast_to([128, n, 128]), view.buf)

        def v3(tile_view_ap, buf, n=H):
            return View(tile_view_ap.rearrange("p (h x) -> p h x", h=n), buf)

        def bfview(pt, ncols):
            return pt.t[:].bitcast(BF16)[:, 0:ncols]

        for b in range(NBLK):
            tsl = slice(b * TB, (b + 1) * TB)
            cx.dma(xT[:], View(srcv[:, :, tsl], src.buf), "ld")
            rms_stats(cx, xT, hT, rstd, tmp)
            for c in range(8):
                cx.stt(hT[:, c, :], xT[:, c, :], cx.vec[:, nwc + c:nwc + c + 1], rstd[:], ALU.mult, ALU.mult)
            for p in range(24):
                kind, h = p // 8, p % 8
                pb = cx.bank()
                for c in range(8):
                    cx.mm(pb[:], Win[:, c, p * 128:(p + 1) * 128], hT[:, c, :], start=(c == 0), stop=(c == 7))
                cx.copy("act", raw[:, 0:3], halo[:, p, :])
                cx.copy("act", raw[:, 3:3 + TB], pb[:])
                cx.copy("act", halo[:, p, :], raw[:, TB:TB + 3])
                cx.ts("pool", acc[:], raw[:, 0:TB], cx.vec[:, cwc + p:cwc + p + 1], ALU.mult, 0.0, ALU.add)
                for k in range(1, 4):
                    cx.stt(acc[:], raw[:, k:k + TB], cx.vec[:, cwc + 24 * k + p:cwc + 24 * k + p + 1], acc[:], ALU.mult, ALU.add)
                if kind == 2:
                    if p == 21:
                        cx.dbg("raw21", raw[:], [128, 3 + TB])
                        cx.dbg("acc21", acc[:], [128, TB])
                    cx.act(VT[:, h, :], acc[:], AF.Silu)
                    if p == 21:
                        cx.dbg("VT21", VT[:, h, :], [128, TB], BF16)
                    continue
                cx.act(sil[:], acc[:], AF.Silu)
                cx.tt("pool", sqb[:], sil[:], sil[:], ALU.mult)
                pss = cx.bank()
                cx.mm(pss[:], C["ones_b"][:], sqb[:])
                if kind == 0:
                    cx.act(rn[:], pss[:], AF.Sqrt, scale=128.0, bias=cc[:, 1:2])
                else:
                    cx.act(rn[:], pss[:], AF.Sqrt, scale=1.0, bias=cx.eps_col[:, 0:1])
                cx.recip(rn[:], rn[:])
                cx.tt("pool", (QT if kind == 0 else KT)[:, h, :], sil[:], rn[:], ALU.mult)
            cx.dbg("hT", hT[:], [128, 8, TB], BF16)
            cx.dbg("QT", QT[:], [128, H, TB], BF16)
            cx.dbg("KT", KT[:], [128, H, TB], BF16)
            cx.dbg("VT", VT[:], [128, H, TB], BF16)
            for s in range(4):
                csl = slice(s * 128, (s + 1) * 128)
                pba = cx.pair()
                for c in range(8):
                    cx.mm(pba[:, 0:16], hT[:, c, csl], Win[:, c, 4096:4112], start=(c == 0), stop=(c == 7))
                cx.act(sm["beta"][:], pba[:, 0:8], AF.Sigmoid)
                cx.ts("pool", sm["nbeta"][:], sm["beta"][:], -1.0, ALU.mult, 0.0, ALU.add)
                cx.tt("dve", sm["x"][:], pba[:, 8:16], DTB[:], ALU.add)
                cx.ts("dve", sm["ax"][:], sm["x"][:], -1.0, ALU.mult)
                cx.tt("dve", sm["ax"][:], sm["ax"][:], sm["x"][:], ALU.min)
                cx.act(sm["e1"][:], sm["ax"][:], AF.Exp)
                cx.act(sm["e1"][:], sm["e1"][:], AF.Ln, bias=cc[:, 0:1])
                cx.stt(sm["g"][:], sm["x"][:], 0.0, sm["e1"][:], ALU.max, ALU.add)
                cx.tt("dve", sm["g"][:], sm["g"][:], negA[:], ALU.mult)
                pG = cx.pair()
                cx.mm(pG[:, 0:8], Tri[:], sm["g"][:])
                cx.mm(pG[:, 8:16], C["ones_f"][:], sm["g"][:])
                cx.act(sm["eG"][:], pG[:, 0:8], AF.Exp)
                cx.act(sm["cd"][:], pG[:, 8:16], AF.Exp)
                cx.tt("dve", sm["bG"][:], sm["eG"][:], sm["beta"][:], ALU.mult)
                cx.copy("dve", sm["ax"][:], pG[:, 0:8])
                cx.tt("dve", sm["k2"][:], pG[:, 8:16], sm["ax"][:], ALU.subtract)
                cx.act(sm["k2"][:], sm["k2"][:], AF.Exp)
                for n_ in ("beta", "g", "eG", "cd", "k2", "bG"):
                    cx.dbg(n_, sm[n_][:], [128, 8])
                cx.tt("dve", gUT[:], bc(sm["g"][:], 128), bch(Ust[:], H), ALU.mult)
                pD = cx.pair()
                for hf in range(2):
                    cx.mm(pD[:, hf * 512:(hf + 1) * 512], Tri[:], View(gUT.t[:, hf * 4:hf * 4 + 4, :].rearrange("p h x -> p (h x)"), gUT.buf))
                cx.act(View(DS.t[:].rearrange("p h x -> p (h x)"), DS.buf), pD[:], AF.Exp)
                cx.aselect(DS[:], DS[:], [[0, H], [-1, 128]], -1, 1, ALU.is_ge)
                cx.tt("dve", gUT[:], bc(sm["g"][:], 128), bch(Tri[:], H), ALU.mult)
                pDt = cx.pair()
                for hf in range(2):
                    cx.mm(pDt[:, hf * 512:(hf + 1) * 512], Ust[:], View(gUT.t[:, hf * 4:hf * 4 + 4, :].rearrange("p h x -> p (h x)"), gUT.buf))
                cx.act(View(DtI.t[:].rearrange("p h x -> p (h x)"), DtI.buf), pDt[:], AF.Exp)
                cx.aselect(DtI[:], DtI[:], [[0, H], [1, 128]], 0, -1, ALU.is_ge)
                cx.dbg("DS", DS[:], [128, H, 128], BF16)
                cx.dbg("DtI", DtI[:], [128, H, 128], BF16)
                pk = cx.pair()
                for h in range(H):
                    cx.tr(View(bfview(pk, 1024)[:, h * 128:(h + 1) * 128], pk.buf), KT[:, h, csl], C["ident_b"][:])
                kv3 = v3(bfview(pk, 1024), pk.buf)
                cx.tt("dve", KG[:], kv3, bc(sm["bG"][:], 128), ALU.mult)
                cx.tt("dve", KG2[:], kv3, bc(sm["k2"][:], 128), ALU.mult)
                pv = cx.pair()
                for h in range(H):
                    cx.tr(View(bfview(pv, 1024)[:, h * 128:(h + 1) * 128], pv.buf), VT[:, h, csl], C["ident_b"][:])
                cx.tt("dve", Vb[:], v3(bfview(pv, 1024), pv.buf), bc(sm["beta"][:], 128), ALU.mult)
                pM = cx.pair()
                for h in range(H):
                    cx.mm(pM[:, h * 128:(h + 1) * 128], KT[:, h, csl], KT[:, h, csl])
                cx.tt("dve", tX[:], v3(pM.t[:], pM.buf), DS[:], ALU.mult)
                cx.tt("pool", Xb[:], tX[:], bc(sm["nbeta"][:], 128), ALU.mult)
                pxt = cx.pair()
                for h in range(H):
                    cx.tr(View(bfview(pxt, 1024)[:, h * 128:(h + 1) * 128], pxt.buf), Xb[:, h, :], C["ident_b"][:])
                cx.copy("act", Zb[:], v3(bfview(pxt, 1024), pxt.buf))
                cx.dbg("X0", Xb[:], [128, H, 128], BF16)
                cx.dbg("Z0", Zb[:], [128, H, 128], BF16)
                cx.dbg("KG", KG[:], [128, H, 128], BF16)
                cx.dbg("Vb", Vb[:], [128, H, 128], BF16)
                cx.tt("dve", Xo[:], Zb[:], bch(MKT1[:], H), ALU.mult)
                cx.tt("pool", Yb[:], Xo[:], I8[:], ALU.add)
                cx.tt("dve", Xo[:], Xb[:], bch(MK[:, 0, :], H), ALU.mult)
                cx.tt("pool", Wl[:], Xo[:], I8[:], ALU.add)
                for lv in range(1, 7):
                    cx.tt("pool", Xo[:], Xb[:], bch(MK[:, lv, :], H), ALU.mult)
                    p1 = cx.pair()
                    for h in range(H):
                        cx.mm(p1[:, h * 128:(h + 1) * 128], Xo[:, h, :], Yb[:, h, :])
                    cx.copy("act", View(P1b.t[:].rearrange("p h x -> p (h x)"), P1b.buf), p1[:])
                    p2 = cx.pair()
                    for h in range(H):
                        cx.mm(p2[:, h * 128:(h + 1) * 128], Wl[:, h, :], P1b[:, h, :])
                    cx.tt("dve", Yb[:], v3(p2.t[:], p2.buf), Yb[:], ALU.add)
                    if lv < 6:
                        pt = cx.pair()
                        for h in range(H):
                            cx.tr(View(bfview(pt, 1024)[:, h * 128:(h + 1) * 128], pt.buf), Yb[:, h, :], C["ident_b"][:])
                        cx.copy("act", Wl[:], v3(bfview(pt, 1024), pt.buf))
                cx.dbg("Y", Yb[:], [128, H, 128], BF16)
                pW = cx.pair()
                for h in range(H):
                    cx.mm(pW[:, h * 128:(h + 1) * 128], KG[:, h, :], Yb[:, h, :])
                cx.act(View(wTn.t[:].rearrange("p h x -> p (h x)"), wTn.buf), pW[:], AF.Copy, scale=-1.0)
                pU = cx.pair()
                for h in range(H):
                    cx.mm(pU[:, h * 128:(h + 1) * 128], Yb[:, h, :], Vb[:, h, :], start=True, stop=False)
                    cx.mm(pU[:, h * 128:(h + 1) * 128], wTn[:, h, :], Sb[:, h, :], start=False, stop=True)
                cx.copy("dve", View(vnew.t[:].rearrange("p h x -> p (h x)"), vnew.buf), pU[:])
                cx.dbg("vnew", vnew[:], [128, H, 128], BF16)
                pP = cx.pair()
                for h in range(H):
                    cx.mm(pP[:, h * 128:(h + 1) * 128], KT[:, h, csl], QT[:, h, csl])
                cx.tt("dve", Pm[:], v3(pP.t[:], pP.buf), DtI[:], ALU.mult)
                pO1 = cx.pair()
                for h in range(H):
                    cx.mm(pO1[:, h * 128:(h + 1) * 128], QT[:, h, csl], Sb[:, h, :])
                pO2 = cx.pair()
                for h in range(H):
                    cx.mm(pO2[:, h * 128:(h + 1) * 128], Pm[:, h, :], vnew[:, h, :])
                cx.copy("act", View(o2s.t[:].rearrange("p h x -> p (h x)"), o2s.buf), pO2[:])
                cx.tt("dve", ot[:], v3(pO1.t[:], pO1.buf), bc(sm["eG"][:], 128), ALU.mult)
                cx.tt("pool", ot[:], ot[:], o2s[:], ALU.add)
                cx.dbg("o", ot[:], [128, H, 128])
                pS = cx.pair()
                for h in range(H):
                    cx.mm(pS[:, h * 128:(h + 1) * 128], KG2[:, h, :], vnew[:, h, :])
                cx.tt("pool", Sf[:], Sf[:], bc(sm["cd"][:], 128), ALU.mult)
                cx.tt("dve", Sf[:], Sf[:], v3(pS.t[:], pS.buf), ALU.add)
                cx.copy("act", Sb[:], Sf[:])
                cx.tt("pool", o2s[:], ot[:], ot[:], ALU.mult)
                cx.kb.op("dve", lambda e, o=sm["ss"], i=o2s: e.tensor_reduce(out=o.t[:, :], in_=i.t[:], axis=AX.X, op=ALU.add),
                         reads=[o2s.buf], writes=[sm["ss"].buf])
                cx.act(sm["rs"][:], sm["ss"][:], AF.Sqrt, scale=1.0 / 128, bias=cx.eps_col[:, 0:1])
                cx.recip(sm["rs"][:], sm["rs"][:])
                pZg = cx.pair()
                for hf in range(2):
                    for c in range(8):
                        cx.mm(pZg[:, hf * 512:(hf + 1) * 512], hT[:, c, csl], Win[:, c, 3072 + hf * 512:3072 + (hf + 1) * 512],
                              start=(c == 0), stop=(c == 7))
                cx.act(View(zs.t[:].rearrange("p h x -> p (h x)"), zs.buf), pZg[:], AF.Silu)
                cx.tt("pool", ot[:], ot[:], bc(sm["rs"][:], 128), ALU.mult)
                cx.tt("pool", of[:], ot[:], zs[:], ALU.mult)
                cx.dbg("of", of[:], [128, H, 128], BF16)
                pOt = cx.pair()
                for h in range(H):
                    cx.tr(View(bfview(pOt, 1024)[:, h * 128:(h + 1) * 128], pOt.buf), of[:, h, :], C["ident_b"][:])
                cx.copy("act", OT[:, :, csl], v3(bfview(pOt, 1024), pOt.buf))
            for m in range(8):
                pb = cx.bank()
                for c in range(8):
                    cx.mm(pb[:], Wout[:, c, m * 128:(m + 1) * 128], OT[:, c, :], start=(c == 0), stop=(c == 7))
                cx.tt("dve", xT[:, m, :], xT[:, m, :], pb[:], ALU.add)
            cx.dma(View(dstv[:, :, tsl], dst.buf), xT[:], "st")
    kb.barrier()
    cx.stack = None
```
